# Optimizing a Trainium2 kernel written in Bass

```python
import math
import jax, jax.numpy as jnp
from jax import lax
import numpy as np

D_MODEL = 1024
BATCH = 8
SEQ = 4096
DEPTH = 4

N_META = 16
CHUNK = 64
META_PAD = CHUNK - N_META
CONV_K = 4
D_FF = 2816
LRU_WIDTH = D_MODEL // 4
LRU_HEADS = 4
LRU_BLOCK = LRU_WIDTH // LRU_HEADS
LRU_C = 8.0
SSD_HEADS = 8
SSD_HEADDIM = 64
SSD_INNER = SSD_HEADS * SSD_HEADDIM
SSD_GROUPS = 2
SSD_HPG = SSD_HEADS // SSD_GROUPS
SSD_STATE = 64
SSD_CONV_DIM = SSD_INNER + 2 * SSD_GROUPS * SSD_STATE
DN_HEADS = 4
DN_DK = 64
DN_DV = 64
DN_CONV_DIM = DN_HEADS * (2 * DN_DK + DN_DV)
MIX_WIDTH = LRU_WIDTH + SSD_INNER + DN_HEADS * DN_DV
IN_SIZES = (LRU_WIDTH, LRU_WIDTH,
            SSD_INNER, SSD_CONV_DIM, SSD_HEADS,
            DN_CONV_DIM, DN_HEADS * DN_DV, DN_HEADS, DN_HEADS)
D_IN = sum(IN_SIZES)
IN_OFFSETS = [int(s) for s in np.cumsum(IN_SIZES)[:-1]]
DEEPNORM_ALPHA = (2 * DEPTH) ** 0.25
DEEPNORM_BETA = (8 * DEPTH) ** -0.25
FFN_RES = 0.5
LN_EPS = 1e-5
RMS_EPS = 1e-6

kernel_name = 'hymba_deepnorm_lru_ssd_deltanet_macaron'

F32 = jnp.float32


def _layer_norm(x, g, b):
    xf = x.astype(F32)
    mu = jnp.mean(xf, axis=-1, keepdims=True)
    var = jnp.mean(jnp.square(xf - mu), axis=-1, keepdims=True)
    return ((xf - mu) * lax.rsqrt(var + LN_EPS) * g.astype(F32) + b.astype(F32)).astype(x.dtype)


def _rms_norm(x, w):
    xf = x.astype(F32)
    return xf * lax.rsqrt(jnp.mean(xf * xf, axis=-1, keepdims=True) + RMS_EPS) * w.astype(F32)


def _l2norm(x):
    return x * lax.rsqrt(jnp.sum(x * x, axis=-1, keepdims=True) + RMS_EPS)


def _causal_conv(x, w, b=None):
    c = x.shape[-1]
    y = lax.conv_general_dilated(x, w[:, None, :].astype(x.dtype), window_strides=(1,),
                                 padding=[(CONV_K - 1, 0)], dimension_numbers=('NWC', 'WIO', 'NWC'),
                                 feature_group_count=c)
    if b is not None:
        y = y + b.astype(x.dtype)
    return y


def _pad_front(x):
    pad = [(0, 0)] * x.ndim
    pad[1] = (META_PAD, 0)
    return jnp.pad(x, pad)


def _swiglu(x, wg, wu, wd):
    return (jax.nn.silu(x @ wg) * (x @ wu)) @ wd


def _segsum(x):
    cs = jnp.cumsum(x, axis=-1)
    n = x.shape[-1]
    mask = jnp.tril(jnp.ones((n, n), dtype=bool))
    return jnp.where(mask, cs[..., :, None] - cs[..., None, :], -jnp.inf)


def _linear_combine(c1, c2):
    a1, b1 = c1
    a2, b2 = c2
    return a1 * a2, a2 * b1 + b2


def _rg_lru_group(u_raw, y_raw, conv_w, conv_b, w_a, b_a, w_x, b_x, lam):
    u = _causal_conv(u_raw, conv_w, conv_b).astype(F32)
    bsz, t, _ = u.shape
    uh = u.reshape(bsz, t, LRU_HEADS, LRU_BLOCK)
    r = jax.nn.sigmoid(jnp.einsum('bthi,hij->bthj', uh, w_a.astype(F32)).reshape(bsz, t, LRU_WIDTH) + b_a.astype(F32))
    i = jax.nn.sigmoid(jnp.einsum('bthi,hij->bthj', uh, w_x.astype(F32)).reshape(bsz, t, LRU_WIDTH) + b_x.astype(F32))
    log_a = -LRU_C * r * jax.nn.softplus(-lam.astype(F32))
    a = jnp.exp(log_a)
    b = jnp.sqrt(-jnp.expm1(2.0 * log_a)) * (i * u)
    _, h = lax.associative_scan(_linear_combine, (a, b), axis=1)
    return (h * jax.nn.gelu(y_raw.astype(F32))).astype(u_raw.dtype)


def _ssd_chunked(x, a, b, c):
    bsz, L, ng, ne, p = x.shape
    nc = L // CHUNK
    x = x.reshape(bsz, nc, CHUNK, ng, ne, p)
    b = b.reshape(bsz, nc, CHUNK, ng, -1)
    c = c.reshape(bsz, nc, CHUNK, ng, -1)
    a = jnp.transpose(a.reshape(bsz, nc, CHUNK, ng, ne), (0, 3, 4, 1, 2))
    a_cs = jnp.cumsum(a, axis=-1)
    l_mat = jnp.exp(_segsum(a))
    y_diag = jnp.einsum('bclgn,bcsgn,bgecls,bcsgep->bclgep', c, b, l_mat, x)
    decay_states = jnp.exp(a_cs[..., -1:] - a_cs)
    states = jnp.einsum('bclgn,bgecl,bclgep->bcgepn', b, decay_states, x)
    states = jnp.concatenate([jnp.zeros_like(states[:, :1]), states], axis=1)
    chunk_tot = jnp.pad(a_cs[..., -1], ((0, 0), (0, 0), (0, 0), (1, 0)))
    decay_chunk = jnp.exp(_segsum(chunk_tot))
    states = jnp.einsum('bgezc,bcgepn->bzgepn', decay_chunk, states)[:, :-1]
    y_off = jnp.einsum('bclgn,bcgepn,bgecl->bclgep', c, states, jnp.exp(a_cs))
    return (y_diag + y_off).reshape(bsz, L, ng, ne, p)


def _ssd_group(z, xbc_raw, dt_raw, conv_w, conv_b, dt_bias, a_log, d_skip, norm_w):
    xbc = jax.nn.silu(_causal_conv(xbc_raw, conv_w, conv_b)).astype(F32)
    bsz, t, _ = xbc.shape
    xs = xbc[..., :SSD_INNER].reshape(bsz, t, SSD_GROUPS, SSD_HPG, SSD_HEADDIM)
    bm = xbc[..., SSD_INNER:SSD_INNER + SSD_GROUPS * SSD_STATE].reshape(bsz, t, SSD_GROUPS, SSD_STATE)
    cm = xbc[..., SSD_INNER + SSD_GROUPS * SSD_STATE:].reshape(bsz, t, SSD_GROUPS, SSD_STATE)
    dt = jax.nn.softplus(dt_raw.astype(F32) + dt_bias.astype(F32)).reshape(bsz, t, SSD_GROUPS, SSD_HPG)
    a = -jnp.exp(a_log.astype(F32)).reshape(SSD_GROUPS, SSD_HPG)
    y = _ssd_chunked(_pad_front(xs * dt[..., None]), _pad_front(dt * a), _pad_front(bm), _pad_front(cm))[:, META_PAD:]
    y = y + d_skip.astype(F32).reshape(SSD_GROUPS, SSD_HPG)[:, :, None] * xs
    gw = SSD_HPG * SSD_HEADDIM
    y = y.reshape(bsz, t, SSD_GROUPS, gw) * jax.nn.silu(z.astype(F32)).reshape(bsz, t, SSD_GROUPS, gw)
    y = _rms_norm(y, norm_w.reshape(SSD_GROUPS, gw))
    return y.reshape(bsz, t, SSD_INNER).astype(z.dtype)


def _gated_delta_chunked(q, k, v, beta, g):
    bsz, nh, L, dk = q.shape
    dv = v.shape[-1]
    nc = L // CHUNK
    q = q.reshape(bsz, nh, nc, CHUNK, dk)
    k = k.reshape(bsz, nh, nc, CHUNK, dk)
    v = v.reshape(bsz, nh, nc, CHUNK, dv)
    beta = beta.reshape(bsz, nh, nc, CHUNK)
    g_cs = jnp.cumsum(g.reshape(bsz, nh, nc, CHUNK), axis=-1)
    incl = jnp.tril(jnp.ones((CHUNK, CHUNK), dtype=bool))
    strict = jnp.tril(jnp.ones((CHUNK, CHUNK), dtype=bool), k=-1)
    decay = jnp.exp(jnp.where(incl, g_cs[..., :, None] - g_cs[..., None, :], -jnp.inf))
    k_beta = k * beta[..., None]
    v_beta = v * beta[..., None]
    m = jnp.where(strict, jnp.einsum('bhnid,bhnjd->bhnij', k_beta, k) * decay, 0.0)
    eye = jnp.eye(CHUNK, dtype=m.dtype)
    rhs = jnp.concatenate([v_beta, k_beta * jnp.exp(g_cs)[..., None]], axis=-1)
    sol = lax.linalg.triangular_solve(eye + m, rhs, left_side=True, lower=True, unit_diagonal=True)
    u, w = sol[..., :dv], sol[..., dv:]
    attn = jnp.where(incl, jnp.einsum('bhnid,bhnjd->bhnij', q, k) * decay, 0.0)
    q_dec = q * jnp.exp(g_cs)[..., None]
    k_dec = k * jnp.exp(g_cs[..., -1:] - g_cs)[..., None]
    chunk_decay = jnp.exp(g_cs[..., -1])

    def step(state, inp):
        qd, kd, uu, ww, aa, cd = inp
        v_new = uu - jnp.einsum('bhcd,bhdv->bhcv', ww, state)
        out = jnp.einsum('bhcd,bhdv->bhcv', qd, state) + jnp.einsum('bhij,bhjv->bhiv', aa, v_new)
        state = state * cd[..., None, None] + jnp.einsum('bhcd,bhcv->bhdv', kd, v_new)
        return state, out

    xs = tuple(jnp.moveaxis(z, 2, 0) for z in (q_dec, k_dec, u, w, attn, chunk_decay))
    s0 = jnp.zeros((bsz, nh, dk, dv), q.dtype)
    _, o = lax.scan(step, s0, xs)
    return jnp.moveaxis(o, 0, 2).reshape(bsz, nh, L, dv)


def _deltanet_group(qkv_raw, gate_raw, beta_raw, alpha_raw, conv_w, a_log, dt_bias, norm_w):
    qkv = jax.nn.silu(_causal_conv(qkv_raw, conv_w)).astype(F32)
    bsz, t, _ = qkv.shape
    nq = DN_HEADS * DN_DK
    q = _l2norm(qkv[..., :nq].reshape(bsz, t, DN_HEADS, DN_DK)) * (DN_DK ** -0.5)
    k = _l2norm(qkv[..., nq:2 * nq].reshape(bsz, t, DN_HEADS, DN_DK))
    v = qkv[..., 2 * nq:].reshape(bsz, t, DN_HEADS, DN_DV)
    beta = jax.nn.sigmoid(beta_raw.astype(F32))
    g = -jnp.exp(a_log.astype(F32)) * jax.nn.softplus(alpha_raw.astype(F32) + dt_bias.astype(F32))
    to_h = lambda z: jnp.swapaxes(_pad_front(z), 1, 2)
    o = _gated_delta_chunked(to_h(q), to_h(k), to_h(v), to_h(beta), to_h(g))
    o = jnp.swapaxes(o, 1, 2)[:, META_PAD:]
    o = _rms_norm(o, norm_w) * jax.nn.silu(gate_raw.astype(F32).reshape(bsz, t, DN_HEADS, DN_DV))
    return o.reshape(bsz, t, DN_HEADS * DN_DV).astype(qkv_raw.dtype)


def setup_inputs(seed: int = 0) -> dict:
    key = jax.random.key(seed)
    ks = jax.random.split(key, 26)
    nrm = lambda k, shape, s: jax.random.normal(k, shape, F32) * s
    unif = lambda k, shape, lo, hi: jax.random.uniform(k, shape, F32, lo, hi)

    def dt_bias_init(k, n):
        dt = jnp.exp(unif(k, (DEPTH, n), math.log(1e-3), math.log(1e-1)))
        return dt + jnp.log(-jnp.expm1(-dt))

    a_c = unif(ks[14], (DEPTH, LRU_WIDTH), 0.9, 0.999)
    s = a_c ** (1.0 / LRU_C)
    return {
        'x': nrm(ks[0], (BATCH, SEQ, D_MODEL), 1.0),
        'meta': nrm(ks[1], (N_META, D_MODEL), 1.0),
        'ln_g': 1.0 + nrm(ks[2], (DEPTH, 3, D_MODEL), 0.02),
        'ln_b': nrm(ks[3], (DEPTH, 3, D_MODEL), 0.02),
        'ffn_w_gate': nrm(ks[4], (DEPTH, 2, D_MODEL, D_FF), D_MODEL ** -0.5),
        'ffn_w_up': nrm(ks[5], (DEPTH, 2, D_MODEL, D_FF), D_MODEL ** -0.5),
        'ffn_w_down': nrm(ks[6], (DEPTH, 2, D_FF, D_MODEL), D_FF ** -0.5 * DEEPNORM_BETA),
        'w_in': nrm(ks[7], (DEPTH, D_MODEL, D_IN), D_MODEL ** -0.5),
        'lru_conv_w': nrm(ks[8], (DEPTH, CONV_K, LRU_WIDTH), CONV_K ** -0.5),
        'lru_conv_b': nrm(ks[9], (DEPTH, LRU_WIDTH), 0.01),
        'lru_w_a': nrm(ks[10], (DEPTH, LRU_HEADS, LRU_BLOCK, LRU_BLOCK), LRU_BLOCK ** -0.5),
        'lru_b_a': nrm(ks[11], (DEPTH, LRU_WIDTH), 0.01),
        'lru_w_x': nrm(ks[12], (DEPTH, LRU_HEADS, LRU_BLOCK, LRU_BLOCK), LRU_BLOCK ** -0.5),
        'lru_b_x': nrm(ks[13], (DEPTH, LRU_WIDTH), 0.01),
        'lru_lambda': jnp.log(s) - jnp.log1p(-s),
        'ssd_conv_w': nrm(ks[15], (DEPTH, CONV_K, SSD_CONV_DIM), CONV_K ** -0.5),
        'ssd_conv_b': nrm(ks[16], (DEPTH, SSD_CONV_DIM), 0.01),
        'ssd_dt_bias': dt_bias_init(ks[17], SSD_HEADS),
        'ssd_a_log': jnp.log(unif(ks[18], (DEPTH, SSD_HEADS), 1.0, 16.0)),
        'ssd_d': 1.0 + nrm(ks[19], (DEPTH, SSD_HEADS), 0.01),
        'ssd_norm_w': 1.0 + nrm(ks[20], (DEPTH, SSD_INNER), 0.01),
        'dn_conv_w': nrm(ks[21], (DEPTH, CONV_K, DN_CONV_DIM), CONV_K ** -0.5),
        'dn_a_log': jnp.log(unif(ks[22], (DEPTH, DN_HEADS), 1.0, 16.0)),
        'dn_dt_bias': dt_bias_init(ks[23], DN_HEADS),
        'dn_norm_w': 1.0 + nrm(ks[24], (DEPTH, DN_DV), 0.01),
        'w_out': nrm(ks[25], (DEPTH, MIX_WIDTH, D_MODEL), MIX_WIDTH ** -0.5 * DEEPNORM_BETA),
    }


def reference(x, meta, ln_g, ln_b, ffn_w_gate, ffn_w_up, ffn_w_down, w_in,
              lru_conv_w, lru_conv_b, lru_w_a, lru_b_a, lru_w_x, lru_b_x, lru_lambda,
              ssd_conv_w, ssd_conv_b, ssd_dt_bias, ssd_a_log, ssd_d, ssd_norm_w,
              dn_conv_w, dn_a_log, dn_dt_bias, dn_norm_w, w_out):
    bsz = x.shape[0]
    h = jnp.concatenate([jnp.broadcast_to(meta.astype(x.dtype)[None], (bsz, N_META, D_MODEL)), x], axis=1)
    for l in range(DEPTH):
        h = _layer_norm(DEEPNORM_ALPHA * h + FFN_RES * _swiglu(h, ffn_w_gate[l, 0], ffn_w_up[l, 0], ffn_w_down[l, 0]),
                        ln_g[l, 0], ln_b[l, 0])
        (lru_u, lru_y, ssd_z, ssd_xbc, ssd_dt, dn_qkv, dn_gate, dn_beta, dn_alpha) = jnp.split(h @ w_in[l], IN_OFFSETS, axis=-1)
        o_lru = _rg_lru_group(lru_u, lru_y, lru_conv_w[l], lru_conv_b[l], lru_w_a[l], lru_b_a[l],
                              lru_w_x[l], lru_b_x[l], lru_lambda[l])
        o_ssd = _ssd_group(ssd_z, ssd_xbc, ssd_dt, ssd_conv_w[l], ssd_conv_b[l], ssd_dt_bias[l],
                           ssd_a_log[l], ssd_d[l], ssd_norm_w[l])
        o_dn = _deltanet_group(dn_qkv, dn_gate, dn_beta, dn_alpha, dn_conv_w[l], dn_a_log[l],
                               dn_dt_bias[l], dn_norm_w[l])
        mix = jnp.concatenate([o_lru, o_ssd, o_dn], axis=-1) @ w_out[l]
        h = _layer_norm(DEEPNORM_ALPHA * h + mix, ln_g[l, 1], ln_b[l, 1])
        h = _layer_norm(DEEPNORM_ALPHA * h + FFN_RES * _swiglu(h, ffn_w_gate[l, 1], ffn_w_up[l, 1], ffn_w_down[l, 1]),
                        ln_g[l, 2], ln_b[l, 2])
    return h[:, N_META:]
```

```python
import contextlib
import math
import numpy as np
import concourse.bass as bass
import concourse.mybir as mybir
from concourse.bass_utils import run_bass_kernel_spmd

F32 = mybir.dt.float32
BF16 = mybir.dt.bfloat16
AF = mybir.ActivationFunctionType
ALU = mybir.AluOpType

D_MODEL = 1024
DEPTH = 4
D_FF = 2816
NFF = D_FF // 128
NKC = D_MODEL // 128
SEQ = 4096
N_META = 16
CH = 64
PADT = SEQ + CH
ALPHA = (2 * DEPTH) ** 0.25
LN_EPS = 1e-5
RMS_EPS = 1e-6
NEG = -30000.0
import os
DBG_STOP = int(os.environ.get('DBG_STOP', '0'))
DBG_SKIP = int(os.environ.get('DBG_SKIP', '0'))

U_GU = 2 * NKC * 128
U_DN = NFF * 128
SLOT = 2816


SEM_LIMIT = int(os.environ.get('SEM_LIMIT', '20000'))


class Src:
    def __init__(self, prog, name, step, multi=False):
        self.prog, self.name, self.step, self.multi = prog, name, step, multi
        self.sems = [prog.new_sem(name + "_e0")]
        self.epoch = 0
        self.cnt = 0

    def signal(self, inst):
        if self.cnt + self.step > SEM_LIMIT:
            self.epoch += 1
            self.sems.append(self.prog.new_sem("%s_e%d" % (self.name, self.epoch)))
            self.cnt = 0
        self.cnt += self.step
        inst.then_inc(self.sems[self.epoch], self.step)
        return (self, self.epoch, self.cnt)


class Buf:
    __slots__ = ("name", "w", "r")

    def __init__(self, name):
        self.name = name
        self.w = None
        self.r = {}


class Prog:
    def __init__(self, nc, es):
        self.nc = nc
        self.es = es
        self.nsem = 0
        self.h = {"pe": nc.tensor, "act": nc.scalar, "dve": nc.vector, "pool": nc.gpsimd, "sp": nc.sync}
        self.src = {}
        for e in self.h:
            self.src[e] = Src(self, e, 1)
        self.waited = {e: {} for e in self.h}
        self.nchan = 0
        self.ninst = 0

    def new_sem(self, name):
        self.nsem += 1
        return self.es.enter_context(self.nc.semaphore("s%d_%s" % (self.nsem, name)))

    def chan(self, multi=False):
        self.nchan += 1
        return Src(self, "ch%d" % self.nchan, 16, multi)

    def _wait(self, eng, dep):
        s, ep, c = dep
        if s.multi:
            ep, c = s.epoch, s.cnt
        w = self.waited[eng]
        if w.get(s.name, (-1, 0)) >= (ep, c):
            return
        self.h[eng].wait_ge(s.sems[ep], c)
        w[s.name] = (ep, c)

    def op(self, eng, fn, reads=(), writes=(), chan=None, sig=True):
        me = self.src[eng]
        deps = []
        for b in reads:
            if b.w is not None:
                deps.append(b.w)
        for b in writes:
            if b.w is not None and b.w[0] is not me:
                deps.append(b.w)
            for d in b.r.values():
                if d[0] is not me:
                    deps.append(d)
        for d in deps:
            self._wait(eng, d)
        inst = fn()
        self.ninst += 1
        if not sig:
            return inst
        s = chan if chan is not None else me
        tok = s.signal(inst)
        for b in writes:
            b.w = tok
            b.r = {}
        for b in reads:
            b.r[s.name] = tok
        return inst

    def mm(self, out, pairs, reads, writes, **kw):
        n = len(pairs)
        for i, (l, r) in enumerate(pairs):
            last = i == n - 1
            self.op("pe", (lambda l=l, r=r, i=i, last=last: self.nc.tensor.matmul(out, l, r, start=(i == 0), stop=last, **kw)),
                    reads=reads if (last or i == 0) else (), writes=writes if (last or i == 0) else (), sig=last)

    def wait_all(self, eng, src):
        for ep, sem in enumerate(src.sems):
            c = src.cnt if ep == src.epoch else (SEM_LIMIT // src.step) * src.step
            if c > 0:
                self.h[eng].wait_ge(sem, c)


class T:
    def __init__(self, P, name, shape, dtype, nsub=1, psum=False):
        self.t = P.es.enter_context((P.nc.psum_tensor if psum else P.nc.sbuf_tensor)(name, list(shape), dtype))
        self.bufs = [Buf("%s.%d" % (name, i)) for i in range(nsub)]
        self.shape = shape

    def __getitem__(self, k):
        return self.t[k]

    def b(self, i=0):
        return self.bufs[i]

    def all(self):
        return list(self.bufs)


def _gu_units(wg, wu):
    g = wg.reshape(NKC, 128, NFF, 128).transpose(2, 1, 0, 3)
    u = wu.reshape(NKC, 128, NFF, 128).transpose(2, 1, 0, 3)
    return np.stack([g, u], axis=2).reshape(NFF, 128, U_GU)


def _dn_units(wd):
    return wd.reshape(NFF, 128, NKC, 128).transpose(2, 1, 0, 3).reshape(NKC, 128, U_DN)


def _col_units(w, ncc):
    a = w.reshape(NKC, 128, ncc, 128).transpose(2, 1, 0, 3)
    return a.reshape(ncc // 2, 2, 128, NKC, 128).transpose(0, 2, 1, 3, 4).reshape(ncc // 2, 128, U_GU)


STAGES_ALL = ("ffn1", "mix", "ffn2")
U_INS = NKC * 16
U_LRU = 4 * 128
N_IN_UNITS = 11
IN_BIG = [(0, 256), (256, 512), (512, 1024), (1024, 1792), (1800, 2568), (2568, 2824)]
IN_SMALL = [(1792, 1800), (2824, 2828), (2828, 2832)]


def stream_layout(stages):
    units = []
    off = 0

    def add(kind, idx, ln):
        nonlocal off
        units.append((kind, idx, off, ln))
        off += ln

    if "ffn1" in stages:
        for m in range(NFF):
            add("gu0", m, U_GU)
        for o in range(NKC):
            add("dn0", o, U_DN)
    if "mix" in stages:
        add("ins", 0, U_INS)
        add("in", 0, U_GU)
        add("in", 1, U_GU)
        add("lru", 0, U_LRU)
        for j in range(2, N_IN_UNITS):
            add("in", j, U_GU)
        for j in range(4):
            add("out", j, U_GU)
    if "ffn2" in stages:
        for m in range(NFF):
            add("gu1", m, U_GU)
        for o in range(NKC):
            add("dn1", o, U_DN)
    return units, off


def host_stream(inp, l, stages):
    units, tot = stream_layout(stages)
    out = np.zeros((128, tot), np.float32)
    first = {}
    for kind, idx, off, ln in units:
        if (kind, idx) not in first:
            first[(kind, idx)] = off
    f32 = lambda a: np.asarray(a, dtype=np.float32)
    for i in range(2):
        if ("gu%d" % i, 0) in first:
            o = first[("gu%d" % i, 0)]
            reg = out[:, o:o + NFF * U_GU].reshape(128, NFF, 2, NKC, 128)
            reg[:, :, 0] = f32(inp["ffn_w_gate"][l, i]).reshape(NKC, 128, NFF, 128).transpose(1, 2, 0, 3)
            reg[:, :, 1] = f32(inp["ffn_w_up"][l, i]).reshape(NKC, 128, NFF, 128).transpose(1, 2, 0, 3)
            o = first[("dn%d" % i, 0)]
            reg = out[:, o:o + NKC * U_DN].reshape(128, NKC, NFF, 128)
            reg[:] = f32(inp["ffn_w_down"][l, i]).reshape(NFF, 128, NKC, 128).transpose(1, 2, 0, 3)
    if ("ins", 0) in first:
        w = f32(inp["w_in"][l])
        big = np.concatenate([w[:, a:b] for a, b in IN_BIG], axis=1)
        bigu = big.reshape(NKC, 128, 11, 2, 128).transpose(1, 2, 3, 0, 4)
        for j in range(N_IN_UNITS):
            o = first[("in", j)]
            out[:, o:o + U_GU] = bigu[:, j].reshape(128, U_GU)
        sm = np.concatenate([w[:, a:b] for a, b in IN_SMALL], axis=1)
        o = first[("ins", 0)]
        out[:, o:o + U_INS] = sm.reshape(NKC, 128, 16).transpose(1, 0, 2).reshape(128, U_INS)
        bd = np.zeros((4, 128, 128), np.float32)
        for g, nm in enumerate(("lru_w_a", "lru_w_x")):
            for pc in range(2):
                for hp in range(2):
                    bd[g * 2 + pc, hp * 64:(hp + 1) * 64, hp * 64:(hp + 1) * 64] = inp[nm][l, pc * 2 + hp]
        o = first[("lru", 0)]
        out[:, o:o + U_LRU] = bd.transpose(1, 0, 2).reshape(128, U_LRU)
        wo = f32(inp["w_out"][l]).reshape(NKC, 128, 4, 2, 128).transpose(1, 2, 3, 0, 4)
        for j in range(4):
            o = first[("out", j)]
            out[:, o:o + U_GU] = wo[:, j].reshape(128, U_GU)
    return out


def ptab_layout():
    cols = {}
    n = 0

    def add(name, k):
        nonlocal n
        cols[name] = n
        n += k

    for l in range(DEPTH):
        for i in range(3):
            add(("ln_g", l, i), NKC)
            add(("ln_b", l, i), NKC)
        add(("lru_cw", l), 8)
        add(("lru_cb", l), 2)
        add(("lru_ba", l), 2)
        add(("lru_bx", l), 2)
        add(("lru_lam", l), 2)
        add(("ssd_cw", l), 32)
        add(("ssd_cb", l), 8)
        add(("ssd_d", l), 4)
        add(("ssd_nw", l), 4)
        add(("dn_cw", l), 40)
        add(("dn_nw", l), 1)
    return cols, n


def _half(v):
    return np.concatenate([v, np.zeros(64, np.float32)])


def _slots_ssd(v):
    return np.stack([v[i * 128:(i + 1) * 128] for i in range(4)] + [_half(v[512 + i * 64:512 + (i + 1) * 64]) for i in range(4)])


def _slots_dn(v):
    return np.stack([_half(v[i * 64:(i + 1) * 64]) for i in range(8)] + [v[512 + i * 128:512 + (i + 1) * 128] for i in range(2)])


def host_ptab(inp):
    cols, n = ptab_layout()
    t = np.zeros((128, n), np.float32)

    def put(name, arr):
        c = cols[name]
        t[:, c:c + arr.shape[0]] = arr.T

    for l in range(DEPTH):
        for i in range(3):
            put(("ln_g", l, i), inp["ln_g"][l, i].reshape(NKC, 128))
            put(("ln_b", l, i), inp["ln_b"][l, i].reshape(NKC, 128))
        put(("lru_cw", l), inp["lru_conv_w"][l].reshape(4 * 2, 128))
        put(("lru_cb", l), inp["lru_conv_b"][l].reshape(2, 128))
        put(("lru_ba", l), inp["lru_b_a"][l].reshape(2, 128))
        put(("lru_bx", l), inp["lru_b_x"][l].reshape(2, 128))
        put(("lru_lam", l), inp["lru_lambda"][l].reshape(2, 128))
        put(("ssd_cw", l), np.stack([_slots_ssd(inp["ssd_conv_w"][l][tap]) for tap in range(4)]).reshape(32, 128))
        put(("ssd_cb", l), _slots_ssd(inp["ssd_conv_b"][l]))
        put(("ssd_d", l), np.repeat(inp["ssd_d"][l], 64).reshape(4, 128))
        put(("ssd_nw", l), inp["ssd_norm_w"][l].reshape(4, 128))
        put(("dn_cw", l), np.stack([_slots_dn(inp["dn_conv_w"][l][tap]) for tap in range(4)]).reshape(40, 128))
        put(("dn_nw", l), np.tile(inp["dn_norm_w"][l], 2).reshape(1, 128))
    return t


RT_W = 28


def host_rtab(inp):
    t = np.zeros((64, DEPTH * RT_W), np.float32)
    for l in range(DEPTH):
        row = np.concatenate([inp["ssd_dt_bias"][l], np.zeros(4, np.float32), inp["dn_dt_bias"][l],
                              inp["ssd_a_log"][l], inp["dn_a_log"][l]]).astype(np.float32)
        t[:, l * RT_W:(l + 1) * RT_W] = row[None, :]
    return t


CF = {"ident": (0, 128), "U2": (128, 64), "U1": (192, 64), "NEGi": (256, 64), "NEGl": (320, 64),
      "C3": (384, 192), "ones": (576, 64), "NEGi8": (640, 512)}
CF_W = 1152
CB = {"ones_mean": (0, 128), "ones256": (128, 128), "blk": (256, 128)}
CB_W = 384


def host_consts():
    f = np.zeros((128, CF_W), np.float32)
    ii = np.arange(64)
    f[:, 0:128] = np.eye(128, dtype=np.float32)
    U2 = (ii[:, None] <= ii[None, :]).astype(np.float32)
    U1 = (ii[:, None] > ii[None, :]).astype(np.float32)
    f[:64, 128:192] = U2
    f[:64, 192:256] = U1
    f[:64, 256:320] = NEG * (ii[:, None] > ii[None, :])
    f[:64, 320:384] = NEG * (ii[:, None] <= ii[None, :])
    f[:64, 384:448] = U1
    f[:64, 448:512] = U2
    f[:64, 512:576] = 1.0
    f[:64, 576:640] = 1.0
    for r in range(8):
        f[:64, 640 + 64 * r:640 + 64 * (r + 1)] = f[:64, 256:320]
    b = np.zeros((128, CB_W), np.float32)
    b[:, 0:128] = 1.0 / D_MODEL
    b[:, 128:256] = 1.0 / 256.0
    b[:64, 256:320] = 1.0
    b[64:, 320:384] = 1.0
    return f, b


class Cfg:
    def __init__(self, n_layers=DEPTH, tile_chunks=(1, 8, 8, 8, 8, 8, 8, 8, 8), stages=STAGES_ALL, nslots=4,
                 dbg_mix=False, mixers=("lru", "ssd", "dn")):
        self.n_layers = n_layers
        self.tile_chunks = tuple(tile_chunks)
        self.stages = tuple(stages)
        self.nslots = nslots
        self.ntok = CH * sum(tile_chunks)
        self.dbg_mix = dbg_mix
        self.mixers = tuple(mixers)


def build(cfg):
    nc = bass.Bass("TRN2", target_bir_lowering=False)
    NT = cfg.ntok
    units, wtot = stream_layout(cfg.stages)
    pcols, pn = ptab_layout()
    xT = nc.dram_tensor("xT", [D_MODEL, NT], F32, kind="ExternalInput").ap()
    wst = [nc.dram_tensor("wst%d" % l, [128, wtot], F32, kind="ExternalInput").ap() for l in range(cfg.n_layers)]
    ptab_d = nc.dram_tensor("ptab", [128, pn], F32, kind="ExternalInput").ap()
    rtab_d = nc.dram_tensor("rtab", [64, DEPTH * RT_W], F32, kind="ExternalInput").ap()
    cf_d = nc.dram_tensor("cf", [128, CF_W], F32, kind="ExternalInput").ap()
    cb_d = nc.dram_tensor("cb", [128, CB_W], F32, kind="ExternalInput").ap()
    yT = nc.dram_tensor("yT", [D_MODEL, NT], F32, kind="ExternalOutput").ap()
    NMAX = CH * max(cfg.tile_chunks)
    NCMAX = max(cfg.tile_chunks)
    ZW = 3 + NMAX
    MUL, ADD, SUB = ALU.mult, ALU.add, ALU.subtract

    with contextlib.ExitStack() as es:
        P = Prog(nc, es)
        V = P.op
        h = T(P, "h", [128, NKC, NMAX], F32, nsub=NKC)
        hbf = T(P, "hbf", [128, NKC, NMAX], BF16, nsub=NKC)
        act = T(P, "act", [128, NFF, NMAX], BF16, nsub=NFF)
        sg = [T(P, "sg%d" % i, [128, NMAX], F32) for i in range(2)]
        tmp = [T(P, "tmp%d" % i, [128, NMAX], F32) for i in range(2)]
        ybf = [T(P, "ybf%d" % i, [128, NMAX], BF16) for i in range(2)]
        ysq = [T(P, "ysq%d" % i, [128, NMAX], BF16) for i in range(2)]
        st_m2 = T(P, "st_m2", [128, NMAX], F32)
        st_var = T(P, "st_var", [128, NMAX], F32)
        st_rstd = T(P, "st_rstd", [128, NMAX], F32)
        st_mean = T(P, "st_mean", [128, NMAX], F32)
        ptab = T(P, "ptab_sb", [128, pn], F32)
        rtab = T(P, "rtab_sb", [64, DEPTH * RT_W], F32)
        dtab = T(P, "dtab_sb", [128, DEPTH * 2], F32)
        drow = T(P, "drow_sb", [64, DEPTH * 12], F32)
        cf = T(P, "cf_sb", [128, CF_W], F32)
        cb = T(P, "cb_sb", [128, CB_W], BF16)
        slots = [T(P, "wslot%d" % i, [128, SLOT], BF16) for i in range(cfg.nslots)]
        slot_ch = [P.chan() for _ in range(cfg.nslots)]
        banks = [T(P, "bank%d" % i, [128, 512], F32, psum=True) for i in range(8)]
        misc_ch = P.chan(multi=True)
        io_ch = P.chan(multi=True)
        rr = {}

        def bank():
            rr["bank"] = rr.get("bank", -1) + 1
            return banks[rr["bank"] % 8]

        def rot(name, lst):
            rr[name] = rr.get(name, -1) + 1
            return lst[rr[name] % len(lst)]

        def cfs(name, rows=64):
            a, w = CF[name]
            return cf[0:rows, a:a + w]

        def cbs(name):
            a, w = CB[name]
            return cb[:, a:a + w]

        ones_mean = cbs("ones_mean")

        V("sp", lambda: nc.sync.dma_start(out=ptab[:, :], in_=ptab_d), writes=ptab.all(), chan=misc_ch)
        V("sp", lambda: nc.sync.dma_start(out=rtab[:, :], in_=rtab_d), writes=rtab.all(), chan=misc_ch)
        V("sp", lambda: nc.sync.dma_start(out=cf[:, :], in_=cf_d), writes=cf.all(), chan=misc_ch)
        V("pool", lambda: nc.gpsimd.dma_start(out=cb[:, :], in_=cb_d), writes=cb.all(), chan=misc_ch)

        mix_on = "mix" in cfg.stages
        if mix_on:
            tail = [T(P, "tail%d" % l, [128, 20, 3], F32) for l in range(cfg.n_layers)]
            hst = [T(P, "hst%d" % l, [128, 2], F32) for l in range(cfg.n_layers)]
            sST = [T(P, "sST%d" % l, [64, 8, 64], F32) for l in range(cfg.n_layers)]
            dS = [T(P, "dS%d" % l, [64, 4, 64], F32) for l in range(cfg.n_layers)]
            for l in range(cfg.n_layers):
                V("pool", lambda l=l: nc.gpsimd.memset(tail[l][:, :, :], 0.0), writes=tail[l].all())
                V("pool", lambda l=l: nc.gpsimd.memset(hst[l][:, :], 0.0), writes=hst[l].all())
                V("pool", lambda l=l: nc.gpsimd.memset(sST[l][:, :, :], 0.0), writes=sST[l].all())
                V("pool", lambda l=l: nc.gpsimd.memset(dS[l][:, :, :], 0.0), writes=dS[l].all())
            for l in range(cfg.n_layers):
                c0 = pcols[("lru_lam", l)]
                V("act", lambda l=l, c0=c0: nc.scalar.activation(out=dtab[:, 2 * l:2 * l + 2], in_=ptab[:, c0:c0 + 2], func=AF.Exp, scale=-1.0),
                  reads=ptab.all(), writes=dtab.all())
                V("dve", lambda l=l: nc.vector.tensor_scalar(out=dtab[:, 2 * l:2 * l + 2], in0=dtab[:, 2 * l:2 * l + 2], scalar1=1.0, scalar2=None, op0=ADD),
                  reads=dtab.all(), writes=dtab.all())
                V("act", lambda l=l: nc.scalar.activation(out=dtab[:, 2 * l:2 * l + 2], in_=dtab[:, 2 * l:2 * l + 2], func=AF.Ln),
                  reads=dtab.all(), writes=dtab.all())
                V("dve", lambda l=l: nc.vector.tensor_scalar(out=dtab[:, 2 * l:2 * l + 2], in0=dtab[:, 2 * l:2 * l + 2], scalar1=-8.0, scalar2=None, op0=MUL),
                  reads=dtab.all(), writes=dtab.all())
                V("act", lambda l=l: nc.scalar.activation(out=drow[:, 12 * l:12 * l + 12], in_=rtab[:, l * RT_W + 16:l * RT_W + 28], func=AF.Exp),
                  reads=rtab.all(), writes=drow.all())
                V("dve", lambda l=l: nc.vector.tensor_scalar(out=drow[:, 12 * l:12 * l + 12], in0=drow[:, 12 * l:12 * l + 12], scalar1=-1.0, scalar2=None, op0=MUL),
                  reads=drow.all(), writes=drow.all())
            zin = T(P, "zin", [128, 12, ZW], F32, nsub=12)
            zs = T(P, "zs", [16, NMAX], F32)
            mixo = T(P, "mixo", [128, NKC, NMAX], BF16, nsub=NKC)
            hc = T(P, "hc", [64, 8, NMAX], F32, nsub=8)
            SCr = T(P, "SCr", [64, NCMAX, 16], F32)
            SCe = T(P, "SCe", [64, NCMAX, 16], F32)
            SC = T(P, "SC", [64, NCMAX, 24], F32)
            CS = T(P, "CS", [64, NCMAX, 24], F32)
            EX = T(P, "EX", [64, NCMAX, 36], F32)
            BS = T(P, "BS", [64, NCMAX, 4], F32)
            BCh = T(P, "BCh", [64, 4, NMAX], BF16)
            qnb = T(P, "qnb", [64, 4, NMAX], BF16)
            knb = T(P, "knb", [64, 4, NMAX], BF16)
            sqb = ybf
            class AV:
                def __init__(self, k):
                    self.k = k
                    self.bufs = [act.b(2 * k), act.b(2 * k + 1)]

                def ap(self, lo=0, hi=NMAX):
                    v = act[:, 2 * self.k:2 * self.k + 2, :].rearrange("p a n -> p (a n)").bitcast(F32)
                    return v[:, lo:hi]

                def all(self):
                    return list(self.bufs)

            cv = [AV(k) for k in range(4)]
            ysb = [AV(4 + k) for k in range(4)]

            def dbl(name, shape, dt):
                return [T(P, "%s%d" % (name, i), shape, dt) for i in range(2)]

            def sgl(name, shape, dt):
                t_ = T(P, name, shape, dt)
                return [t_, t_]
            xdt = dbl("xdt", [64, 8, 64], BF16)
            xdec = sgl("xdec", [64, 8, 64], BF16)
            Btm = dbl("Btm", [64, 128], BF16)
            GU2 = T(P, "GU2", [64, 8, 64], F32)
            LT = T(P, "LT", [64, 8, 64], F32)
            Eb = T(P, "Eb", [64, 8, 64], F32)
            MT = dbl("MT", [64, 8, 64], BF16)
            Cd = sgl("Cd", [64, 8, 64], BF16)
            STb = sgl("STb", [64, 8, 64], BF16)
            kbd = sgl("kbd", [64, 4, 64], BF16)
            kdec = sgl("kdec", [64, 4, 64], BF16)
            vb = sgl("vb", [64, 4, 64], BF16)
            GUd2 = T(P, "GUd2", [64, 4, 64], F32)
            DEC = T(P, "DEC", [64, 8, 64], F32)
            DECb = T(P, "DECb", [64, 4, 64], F32)
            EBd = T(P, "EBd", [64, 4, 64], F32)
            mf = T(P, "mf", [64, 4, 64], F32)
            PQ = dbl("PQ", [64, 4, 128], BF16)
            Xk = dbl("Xk", [64, 4, 64], BF16)
            attnT = sgl("attnT", [64, 4, 64], BF16)
            qdT = sgl("qdT", [64, 4, 64], BF16)
            usb = T(P, "usb", [64, 4, 64], F32)
            wTb = sgl("wTb", [64, 4, 64], BF16)
            vnew = dbl("vnew", [64, 4, 64], BF16)
            Sb = dbl("Sb", [64, 4, 64], BF16)

        seq = []
        for ti in range(len(cfg.tile_chunks)):
            for l in range(cfg.n_layers):
                for u in units:
                    seq.append((l,) + u)
        wstate = {"next": 0, "issued": 0}

        def wnext(kind, idx, l):
            u = wstate["next"]
            assert seq[u][0] == l and seq[u][1] == kind and seq[u][2] == idx, (seq[u], kind, idx, l)
            lim = min(u + cfg.nslots, len(seq))
            while wstate["issued"] < lim:
                v = wstate["issued"]
                ll, kk, ii, off, ln = seq[v]
                sl = slots[v % cfg.nslots]
                V("pool", (lambda sl=sl, ll=ll, off=off, ln=ln: nc.gpsimd.dma_start(out=sl[:, 0:ln], in_=wst[ll][:, off:off + ln])),
                  writes=sl.all(), chan=slot_ch[v % cfg.nslots])
                wstate["issued"] += 1
            wstate["next"] += 1
            return slots[u % cfg.nslots]

        def layer_norm(l, i, N):
            pm = bank()
            pq = bank()
            for c in range(NKC):
                a = rot("ybf", ybf)
                q = ysq[rr["ybf"] % 2]
                V("act", lambda a=a, c=c: nc.scalar.copy(out=a[:, :N], in_=h[:, c, :N]), reads=[h.b(c)], writes=a.all())
                V("act", lambda q=q, c=c: nc.scalar.activation(out=q[:, :N], in_=h[:, c, :N], func=AF.Square), reads=[h.b(c)], writes=q.all())
                V("pe", lambda a=a, c=c: nc.tensor.matmul(pm[:, :N], ones_mean, a[:, :N], start=(c == 0), stop=(c == NKC - 1)),
                  reads=a.all() + cb.all(), writes=pm.all())
                V("pe", lambda q=q, c=c: nc.tensor.matmul(pq[:, :N], ones_mean, q[:, :N], start=(c == 0), stop=(c == NKC - 1)),
                  reads=q.all(), writes=pq.all())
            V("act", lambda: nc.scalar.activation(out=st_m2[:, :N], in_=pm[:, :N], func=AF.Square), reads=pm.all(), writes=st_m2.all())
            V("act", lambda: nc.scalar.copy(out=st_mean[:, :N], in_=pm[:, :N]), reads=pm.all(), writes=st_mean.all())
            V("dve", lambda: nc.vector.scalar_tensor_tensor(out=st_var[:, :N], in0=st_m2[:, :N], scalar=-1.0, op0=MUL, in1=pq[:, :N], op1=ADD),
              reads=st_m2.all() + pq.all(), writes=st_var.all())
            V("dve", lambda: nc.vector.tensor_scalar(out=st_var[:, :N], in0=st_var[:, :N], scalar1=LN_EPS, scalar2=None, op0=ADD),
              reads=st_var.all(), writes=st_var.all())
            V("act", lambda: nc.scalar.activation(out=st_var[:, :N], in_=st_var[:, :N], func=AF.Ln), reads=st_var.all(), writes=st_var.all())
            V("act", lambda: nc.scalar.activation(out=st_rstd[:, :N], in_=st_var[:, :N], func=AF.Exp, scale=-0.5), reads=st_var.all(), writes=st_rstd.all())
            cg = pcols[("ln_g", l, i)]
            cbb = pcols[("ln_b", l, i)]
            for c in range(NKC):
                t = rot("tmp", tmp)
                V("dve", lambda t=t, c=c: nc.vector.tensor_tensor(out=t[:, :N], in0=h[:, c, :N], in1=st_mean[:, :N], op=SUB),
                  reads=[h.b(c)] + st_mean.all(), writes=t.all())
                V("dve", lambda t=t: nc.vector.tensor_tensor(out=t[:, :N], in0=t[:, :N], in1=st_rstd[:, :N], op=MUL),
                  reads=t.all() + st_rstd.all(), writes=t.all())
                V("act", lambda t=t, c=c: nc.scalar.activation(out=h[:, c, :N], in_=t[:, :N], func=AF.Identity,
                                                                scale=ptab[:, cg + c:cg + c + 1], bias=ptab[:, cbb + c:cbb + c + 1]),
                  reads=t.all() + ptab.all(), writes=[h.b(c)])
                V("act", lambda c=c: nc.scalar.copy(out=hbf[:, c, :N], in_=h[:, c, :N]), reads=[h.b(c)], writes=[hbf.b(c)])

        def ffn(l, i, N):
            for m in range(NFF):
                w = wnext("gu%d" % i, m, l)
                pg = bank()
                pu = bank()
                P.mm(pg[:, :N], [(w[:, kc * 128:(kc + 1) * 128], hbf[:, kc, :N]) for kc in range(NKC)],
                     reads=w.all() + hbf.all(), writes=pg.all())
                P.mm(pu[:, :N], [(w[:, (NKC + kc) * 128:(NKC + kc + 1) * 128], hbf[:, kc, :N]) for kc in range(NKC)],
                     reads=w.all() + hbf.all(), writes=pu.all())
                s = rot("sg", sg)
                V("act", lambda s=s, pg=pg: nc.scalar.activation(out=s[:, :N], in_=pg[:, :N], func=AF.Silu), reads=pg.all(), writes=s.all())
                V("dve", lambda s=s, pu=pu, m=m: nc.vector.scalar_tensor_tensor(out=act[:, m, :N], in0=s[:, :N], scalar=0.5, op0=MUL,
                                                                               in1=pu[:, :N], op1=MUL),
                  reads=s.all() + pu.all(), writes=[act.b(m)])
            for o in range(NKC):
                w = wnext("dn%d" % i, o, l)
                pd = bank()
                P.mm(pd[:, :N], [(w[:, kc * 128:(kc + 1) * 128], act[:, kc, :N]) for kc in range(NFF)],
                     reads=w.all() + act.all(), writes=pd.all())
                V("dve", lambda pd=pd, o=o: nc.vector.scalar_tensor_tensor(out=h[:, o, :N], in0=h[:, o, :N], scalar=ALPHA, op0=MUL,
                                                                          in1=pd[:, :N], op1=ADD),
                  reads=[h.b(o)] + pd.all(), writes=[h.b(o)])
            layer_norm(l, 0 if i == 0 else 2, N)

        def inproj(l, N, ti, spec, tail_base):
            tj = tail_base
            for uid, ccs in spec:
                w = wnext("in", uid, l)
                for cc, ent in enumerate(ccs):
                    subs = [(ent[0], 0, 128, ent[1])] if len(ent) == 2 else [(ent[0], 0, 64, ent[2]), (ent[1], 64, 64, ent[2])]
                    for slot, coff, M, is_conv in subs:
                        pb = bank()
                        P.mm(pb[0:M, :N], [(w[:, (cc * NKC + kc) * 128 + coff:(cc * NKC + kc) * 128 + coff + M], hbf[:, kc, :N]) for kc in range(NKC)],
                             reads=w.all() + hbf.all(), writes=pb.all())
                        rr["ev"] = rr.get("ev", 0) + 1
                        if rr["ev"] % 2 == 0:
                            V("act", lambda: nc.scalar.copy(out=zin[0:M, slot, 3:3 + N], in_=pb[0:M, :N]), reads=pb.all(), writes=[zin.b(slot)])
                        else:
                            V("dve", lambda: nc.vector.tensor_copy(out=zin[0:M, slot, 3:3 + N], in_=pb[0:M, :N]), reads=pb.all(), writes=[zin.b(slot)])
                        if is_conv:
                            if ti == 0:
                                V("dve", lambda: nc.vector.memset(zin[0:M, slot, 3:3 + 48], 0.0), writes=[zin.b(slot)])
                            V("dve", lambda: nc.vector.tensor_copy(out=zin[0:M, slot, 0:3], in_=tail[l][0:M, tj, :]),
                              reads=tail[l].all(), writes=[zin.b(slot)])
                            V("dve", lambda: nc.vector.tensor_copy(out=tail[l][0:M, tj, :], in_=zin[0:M, slot, N:N + 3]),
                              reads=[zin.b(slot)], writes=tail[l].all())
                            tj += 1

        def conv4(l, key, ncol, j, k, M, N, out_ap, out_bufs, bias_key=None):
            c0 = pcols[(key, l)]
            w = lambda tap: ptab[0:M, c0 + tap * ncol + j:c0 + tap * ncol + j + 1]
            if bias_key is not None:
                cbk = pcols[(bias_key, l)] + j
                V("act", lambda: nc.scalar.activation(out=out_ap, in_=zin[0:M, k, 3:3 + N], func=AF.Identity, scale=w(3), bias=ptab[0:M, cbk:cbk + 1]),
                  reads=[zin.b(k)] + ptab.all(), writes=out_bufs)
            else:
                V("act", lambda: nc.scalar.activation(out=out_ap, in_=zin[0:M, k, 3:3 + N], func=AF.Identity, scale=w(3)),
                  reads=[zin.b(k)] + ptab.all(), writes=out_bufs)
            for tap in range(3):
                V("dve", lambda tap=tap: nc.vector.scalar_tensor_tensor(out=out_ap, in0=zin[0:M, k, tap:tap + N], scalar=w(tap), op0=MUL,
                                                                         in1=out_ap, op1=ADD),
                  reads=[zin.b(k)] + out_bufs, writes=out_bufs)

        def rstd_from(ps_ap, ps_bufs, eps, outT, N, scale=1.0, M=128):
            V("dve", lambda: nc.vector.tensor_scalar(out=outT[0:M, :N], in0=ps_ap, scalar1=scale, scalar2=eps, op0=MUL, op1=ADD),
              reads=ps_bufs, writes=outT.all())
            V("act", lambda: nc.scalar.activation(out=outT[0:M, :N], in_=outT[0:M, :N], func=AF.Ln), reads=outT.all(), writes=outT.all())
            V("act", lambda: nc.scalar.activation(out=outT[0:M, :N], in_=outT[0:M, :N], func=AF.Exp, scale=-0.5), reads=outT.all(), writes=outT.all())

        def small_scalars(l, N, nch, ti):
            w = wnext("ins", 0, l)
            pb = bank()
            P.mm(pb[0:16, :N], [(w[:, kc * 16:(kc + 1) * 16], hbf[:, kc, :N]) for kc in range(NKC)],
                 reads=w.all() + hbf.all(), writes=pb.all())
            V("act", lambda: nc.scalar.copy(out=zs[:, :N], in_=pb[0:16, :N]), reads=pb.all(), writes=zs.all())
            pt = bank()
            for c in range(nch):
                V("pe", lambda c=c: nc.tensor.transpose(pt[0:64, c * 16:(c + 1) * 16], zs[0:16, c * 64:(c + 1) * 64], cf[0:16, 0:16]),
                  reads=zs.all() + cf.all(), writes=pt.all())
            ptv = pt[0:64, 0:nch * 16].rearrange("p (c w) -> p c w", w=16)
            rb = rtab[:, l * RT_W:l * RT_W + 16].unsqueeze(1).broadcast_to([64, nch, 16])
            V("dve", lambda: nc.vector.tensor_tensor(out=SCr[:, :nch, :], in0=ptv, in1=rb, op=ADD), reads=pt.all() + rtab.all(), writes=SCr.all())
            V("act", lambda: nc.scalar.activation(out=SCe[:, :nch, :], in_=SCr[:, :nch, :], func=AF.Exp), reads=SCr.all(), writes=SCe.all())
            V("dve", lambda: nc.vector.tensor_scalar(out=SCe[:, :nch, :], in0=SCe[:, :nch, :], scalar1=1.0, scalar2=None, op0=ADD), reads=SCe.all(), writes=SCe.all())
            V("act", lambda: nc.scalar.activation(out=SCr[:, :nch, :], in_=SCe[:, :nch, :], func=AF.Ln), reads=SCe.all(), writes=SCr.all())
            V("dve", lambda: nc.vector.reciprocal(out=SCe[:, :nch, 8:12], in_=SCe[:, :nch, 8:12]), reads=SCe.all(), writes=SCe.all())
            V("dve", lambda: nc.vector.tensor_scalar(out=SC[:, :nch, 20:24], in0=SCe[:, :nch, 8:12], scalar1=-1.0, scalar2=1.0, op0=MUL, op1=ADD),
              reads=SCe.all(), writes=SC.all())
            V("dve", lambda: nc.vector.tensor_copy(out=SC[:, :nch, 0:8], in_=SCr[:, :nch, 0:8]), reads=SCr.all(), writes=SC.all())
            ar = drow[:, 12 * l:12 * l + 8].unsqueeze(1).broadcast_to([64, nch, 8])
            gr_ = drow[:, 12 * l + 8:12 * l + 12].unsqueeze(1).broadcast_to([64, nch, 4])
            V("dve", lambda: nc.vector.tensor_tensor(out=SC[:, :nch, 8:16], in0=SCr[:, :nch, 0:8], in1=ar, op=MUL), reads=SCr.all() + drow.all(), writes=SC.all())
            V("dve", lambda: nc.vector.tensor_tensor(out=SC[:, :nch, 16:20], in0=SCr[:, :nch, 12:16], in1=gr_, op=MUL), reads=SCr.all() + drow.all(), writes=SC.all())
            if ti == 0:
                V("dve", lambda: nc.vector.memset(SC[0:48, 0, 0:16], 0.0), writes=SC.all())
                V("dve", lambda: nc.vector.memset(SC[0:48, 0, 20:24], 0.0), writes=SC.all())
            pc_ = bank()
            for c in range(nch):
                V("pe", lambda c=c: nc.tensor.matmul(pc_[0:64, c * 24:c * 24 + 12], cfs("U2"), SC[:, c, 8:20], start=True, stop=True),
                  reads=SC.all() + cf.all(), writes=pc_.all())
                V("pe", lambda c=c: nc.tensor.matmul(pc_[0:64, c * 24 + 12:c * 24 + 24], cfs("ones"), SC[:, c, 8:20], start=True, stop=True),
                  reads=SC.all(), writes=pc_.all())
            pcv = pc_[0:64, 0:nch * 24].rearrange("p (c w) -> p c w", w=24)
            V("act", lambda: nc.scalar.copy(out=CS[:, :nch, :], in_=pcv), reads=pc_.all(), writes=CS.all())
            V("act", lambda: nc.scalar.activation(out=EX[:, :nch, 0:12], in_=CS[:, :nch, 0:12], func=AF.Exp), reads=CS.all(), writes=EX.all())
            V("act", lambda: nc.scalar.activation(out=EX[:, :nch, 24:36], in_=CS[:, :nch, 12:24], func=AF.Exp), reads=CS.all(), writes=EX.all())
            V("dve", lambda: nc.vector.tensor_tensor(out=CS[:, :nch, 0:12], in0=CS[:, :nch, 12:24], in1=CS[:, :nch, 0:12], op=SUB), reads=CS.all(), writes=CS.all())
            V("act", lambda: nc.scalar.activation(out=EX[:, :nch, 12:24], in_=CS[:, :nch, 0:12], func=AF.Exp), reads=CS.all(), writes=EX.all())
            V("dve", lambda: nc.vector.tensor_tensor(out=BS[:, :nch, :], in0=SC[:, :nch, 20:24], in1=EX[:, :nch, 8:12], op=MUL), reads=SC.all() + EX.all(), writes=BS.all())

        def mixer_lru(l, N, ti):
            inproj(l, N, ti, [(0, [(0, True), (1, True)]), (1, [(2, False), (3, False)])], 0)
            wl = wnext("lru", 0, l)
            if "lru" not in cfg.mixers:
                return
            K2 = 2.0 * math.sqrt(2.0 / math.pi)
            for pc in range(2):
                u = cv[pc]
                conv4(l, "lru_cw", 2, pc, pc, 128, N, u.ap(0, N), u.all(), bias_key="lru_cb")
                ub = rot("sqb", sqb)
                V("act", lambda: nc.scalar.copy(out=ub[:, :N], in_=u.ap(0, N)), reads=u.all(), writes=ub.all())
                pa = bank()
                px = bank()
                V("pe", lambda: nc.tensor.matmul(pa[:, :N], wl[:, pc * 128:(pc + 1) * 128], ub[:, :N], start=True, stop=True),
                  reads=wl.all() + ub.all(), writes=pa.all())
                V("pe", lambda: nc.tensor.matmul(px[:, :N], wl[:, (2 + pc) * 128:(3 + pc) * 128], ub[:, :N], start=True, stop=True),
                  reads=wl.all() + ub.all(), writes=px.all())
                r_ = st_m2
                i_ = st_var
                a_ = st_rstd
                b_ = st_mean
                cba = pcols[("lru_ba", l)] + pc
                cbx = pcols[("lru_bx", l)] + pc
                V("act", lambda: nc.scalar.activation(out=r_[:, :N], in_=pa[:, :N], func=AF.Sigmoid, bias=ptab[:, cba:cba + 1]), reads=pa.all() + ptab.all(), writes=r_.all())
                V("act", lambda: nc.scalar.activation(out=i_[:, :N], in_=px[:, :N], func=AF.Sigmoid, bias=ptab[:, cbx:cbx + 1]), reads=px.all() + ptab.all(), writes=i_.all())
                V("act", lambda: nc.scalar.activation(out=a_[:, :N], in_=r_[:, :N], func=AF.Exp, scale=dtab[:, 2 * l + pc:2 * l + pc + 1]), reads=r_.all() + dtab.all(), writes=a_.all())
                V("dve", lambda: nc.vector.tensor_tensor(out=r_[:, :N], in0=a_[:, :N], in1=a_[:, :N], op=MUL), reads=a_.all(), writes=r_.all())
                V("dve", lambda: nc.vector.tensor_scalar(out=r_[:, :N], in0=r_[:, :N], scalar1=-1.0, scalar2=1.0, op0=MUL, op1=ADD), reads=r_.all(), writes=r_.all())
                V("act", lambda: nc.scalar.activation(out=r_[:, :N], in_=r_[:, :N], func=AF.Sqrt), reads=r_.all(), writes=r_.all())
                V("dve", lambda: nc.vector.tensor_tensor(out=i_[:, :N], in0=i_[:, :N], in1=u.ap(0, N), op=MUL), reads=i_.all() + u.all(), writes=i_.all())
                V("dve", lambda: nc.vector.tensor_tensor(out=b_[:, :N], in0=i_[:, :N], in1=r_[:, :N], op=MUL), reads=i_.all() + r_.all(), writes=b_.all())
                if ti == 0:
                    V("dve", lambda: nc.vector.memset(b_[:, 0:48], 0.0), writes=b_.all())
                if DBG_STOP == 1:
                    continue
                hs = tmp[pc]
                V("dve", lambda: nc.vector.tensor_tensor_scan(out=hs[:, :N], data0=a_[:, :N], data1=b_[:, :N], initial=hst[l][:, pc:pc + 1], op0=MUL, op1=ADD),
                  reads=a_.all() + b_.all() + hst[l].all(), writes=hs.all())
                V("act", lambda: nc.scalar.copy(out=hst[l][:, pc:pc + 1], in_=hs[:, N - 1:N]), reads=hs.all(), writes=hst[l].all())
                if DBG_STOP == 2:
                    continue
                yv = zin[:, 2 + pc, 3:3 + N]
                g1 = sg[pc]
                V("dve", lambda: nc.vector.tensor_tensor(out=g1[:, :N], in0=yv, in1=yv, op=MUL), reads=[zin.b(2 + pc)], writes=g1.all())
                V("dve", lambda: nc.vector.tensor_scalar(out=g1[:, :N], in0=g1[:, :N], scalar1=0.044715, scalar2=1.0, op0=MUL, op1=ADD), reads=g1.all(), writes=g1.all())
                V("dve", lambda: nc.vector.tensor_tensor(out=g1[:, :N], in0=g1[:, :N], in1=yv, op=MUL), reads=g1.all() + [zin.b(2 + pc)], writes=g1.all())
                V("act", lambda: nc.scalar.activation(out=g1[:, :N], in_=g1[:, :N], func=AF.Sigmoid, scale=K2), reads=g1.all(), writes=g1.all())
                V("dve", lambda: nc.vector.tensor_tensor(out=g1[:, :N], in0=g1[:, :N], in1=yv, op=MUL), reads=g1.all() + [zin.b(2 + pc)], writes=g1.all())
                V("dve", lambda: nc.vector.tensor_tensor(out=mixo[:, pc, :N], in0=g1[:, :N], in1=hs[:, :N], op=MUL), reads=g1.all() + hs.all(), writes=[mixo.b(pc)])

        def mixer_ssd(l, N, nch, ti):
            inproj(l, N, ti, [(2, [(0, False), (1, False)]), (3, [(2, False), (3, False)]),
                              (4, [(4, True), (5, True)]), (5, [(6, True), (7, True)]),
                              (6, [(8, 9, True), (10, 11, True)])], 2)
            if "ssd" not in cfg.mixers:
                return
            for j in range(4):
                o = cv[j]
                conv4(l, "ssd_cw", 8, j, 4 + j, 128, N, o.ap(0, N), o.all(), bias_key="ssd_cb")
                V("act", lambda: nc.scalar.activation(out=o.ap(0, N), in_=o.ap(0, N), func=AF.Silu), reads=o.all(), writes=o.all())
            for j in range(4):
                conv4(l, "ssd_cw", 8, 4 + j, 8 + j, 64, N, hc[:, j, :N], [hc.b(j)], bias_key="ssd_cb")
                V("act", lambda: nc.scalar.activation(out=hc[:, j, :N], in_=hc[:, j, :N], func=AF.Silu), reads=[hc.b(j)], writes=[hc.b(j)])
                V("act", lambda: nc.scalar.copy(out=BCh[:, j, :N], in_=hc[:, j, :N]), reads=[hc.b(j)], writes=BCh.all())
            ST = sST[l]
            id64 = cf[0:64, 0:64]
            for c in range(nch):
                cs_ = slice(c * 64, (c + 1) * 64)
                par = c % 2
                bx = bank()
                for pc in range(4):
                    V("pe", lambda pc=pc: nc.tensor.transpose(bx[0:64, pc * 128:(pc + 1) * 128], cv[pc].ap(c * 64, (c + 1) * 64), cf[:, 0:128]),
                      reads=cv[pc].all() + cf.all(), writes=bx.all())
                bb = bank()
                for g in range(2):
                    V("pe", lambda g=g: nc.tensor.transpose(bb[0:64, g * 64:(g + 1) * 64], hc[:, g, cs_], id64), reads=[hc.b(g)] + cf.all(), writes=bb.all())
                xd, xe, bt = xdt[par], xdec[par], Btm[par]
                bxv = bx[0:64, :].rearrange("p (e q) -> p e q", q=64)
                V("dve", lambda: nc.vector.tensor_tensor(out=xd[:, :, :], in0=bxv, in1=SC[:, c, 0:8].unsqueeze(2).broadcast_to([64, 8, 64]), op=MUL), reads=bx.all() + SC.all(), writes=xd.all())
                V("dve", lambda: nc.vector.tensor_tensor(out=xe[:, :, :], in0=xd[:, :, :], in1=EX[:, c, 12:20].unsqueeze(2).broadcast_to([64, 8, 64]), op=MUL), reads=xd.all() + EX.all(), writes=xe.all())
                V("act", lambda: nc.scalar.copy(out=bt[:, :], in_=bb[0:64, 0:128]), reads=bb.all(), writes=bt.all())
                V("dve", lambda: nc.vector.tensor_tensor(out=GU2[:, :, :], in0=cfs("U2").unsqueeze(1).broadcast_to([64, 8, 64]), in1=SC[:, c, 8:16].unsqueeze(2).broadcast_to([64, 8, 64]), op=MUL),
                  reads=cf.all() + SC.all(), writes=GU2.all())
                g2f = GU2[:, :, :].rearrange("p e q -> p (e q)")
                bd = bank()
                be = bank()
                P.mm(bd[0:64, :], [(cfs("U1"), g2f), (id64, cfs("NEGi8"))], reads=GU2.all() + cf.all(), writes=bd.all())
                V("pe", lambda: nc.tensor.matmul(be[0:64, :], cfs("ones"), g2f, start=True, stop=True), reads=GU2.all() + cf.all(), writes=be.all())
                V("act", lambda: nc.scalar.activation(out=LT[:, :, :], in_=bd[0:64, :].rearrange("p (e q) -> p e q", q=64), func=AF.Exp), reads=bd.all(), writes=LT.all())
                V("act", lambda: nc.scalar.activation(out=Eb[:, :, :], in_=be[0:64, :].rearrange("p (e q) -> p e q", q=64), func=AF.Exp), reads=be.all(), writes=Eb.all())
                bg = bank()
                for g in range(2):
                    V("pe", lambda g=g: nc.tensor.matmul(bg[0:64, g * 64:(g + 1) * 64], BCh[:, g, cs_], BCh[:, 2 + g, cs_], start=True, stop=True), reads=BCh.all(), writes=bg.all())
                mt, cd = MT[par], Cd[par]
                for g in range(2):
                    V("dve", lambda g=g: nc.vector.tensor_tensor(out=mt[:, 4 * g:4 * g + 4, :], in0=LT[:, 4 * g:4 * g + 4, :], in1=bg[0:64, g * 64:(g + 1) * 64].unsqueeze(1).broadcast_to([64, 4, 64]), op=MUL),
                      reads=LT.all() + bg.all(), writes=mt.all())
                    V("dve", lambda g=g: nc.vector.tensor_tensor(out=cd[:, 4 * g:4 * g + 4, :], in0=Eb[:, 4 * g:4 * g + 4, :], in1=hc[:, 2 + g, cs_].unsqueeze(1).broadcast_to([64, 4, 64]), op=MUL),
                      reads=Eb.all() + [hc.b(2 + g)], writes=cd.all())
                stb = STb[par]
                V("act", lambda: nc.scalar.copy(out=stb[:, :, :], in_=ST[:, :, :]), reads=ST.all(), writes=stb.all())
                by = bank()
                for e in range(8):
                    pc, hp = e // 2, e % 2
                    P.mm(by[64 * hp:64 * hp + 64, pc * 64:(pc + 1) * 64], [(xd[:, e, :], mt[:, e, :]), (stb[:, e, :], cd[:, e, :])],
                         reads=xd.all() + mt.all() + stb.all() + cd.all(), writes=by.all())
                for pc in range(4):
                    V("act", lambda pc=pc: nc.scalar.copy(out=ysb[pc].ap(c * 64, (c + 1) * 64), in_=by[:, pc * 64:(pc + 1) * 64]), reads=by.all(), writes=ysb[pc].all())
                bs = bank()
                for g in range(2):
                    V("pe", lambda g=g: nc.tensor.matmul(bs[0:64, g * 256:(g + 1) * 256], bt[:, g * 64:(g + 1) * 64], xe[:, 4 * g:4 * g + 4, :].rearrange("p e q -> p (e q)"), start=True, stop=True),
                      reads=bt.all() + xe.all(), writes=bs.all())
                V("dve", lambda: nc.vector.tensor_tensor(out=ST[:, :, :], in0=ST[:, :, :], in1=Eb[:, :, 63:64].broadcast_to([64, 8, 64]), op=MUL), reads=ST.all() + Eb.all(), writes=ST.all())
                V("dve", lambda: nc.vector.tensor_tensor(out=ST[:, :, :], in0=ST[:, :, :], in1=bs[0:64, :].rearrange("p (e q) -> p e q", q=64), op=ADD), reads=ST.all() + bs.all(), writes=ST.all())
            cd_ = pcols[("ssd_d", l)]
            cn_ = pcols[("ssd_nw", l)]
            for pc in range(4):
                y_ = ysb[pc]
                V("dve", lambda: nc.vector.scalar_tensor_tensor(out=y_.ap(0, N), in0=cv[pc].ap(0, N), scalar=ptab[:, cd_ + pc:cd_ + pc + 1], op0=MUL, in1=y_.ap(0, N), op1=ADD),
                  reads=cv[pc].all() + y_.all() + ptab.all(), writes=y_.all())
                V("act", lambda: nc.scalar.activation(out=zin[:, pc, 3:3 + N], in_=zin[:, pc, 3:3 + N], func=AF.Silu), reads=[zin.b(pc)], writes=[zin.b(pc)])
                V("dve", lambda: nc.vector.tensor_tensor(out=y_.ap(0, N), in0=y_.ap(0, N), in1=zin[:, pc, 3:3 + N], op=MUL), reads=y_.all() + [zin.b(pc)], writes=y_.all())
            for g in range(2):
                pr = bank()
                for k in range(2):
                    pc = 2 * g + k
                    q = rot("sqb", sqb)
                    V("act", lambda: nc.scalar.activation(out=q[:, :N], in_=ysb[pc].ap(0, N), func=AF.Square), reads=ysb[pc].all(), writes=q.all())
                    V("pe", lambda: nc.tensor.matmul(pr[:, :N], cbs("ones256"), q[:, :N], start=(k == 0), stop=(k == 1)), reads=q.all() + cb.all(), writes=pr.all())
                rs = tmp[g]
                rstd_from(pr[:, :N], pr.all(), RMS_EPS, rs, N)
                for k in range(2):
                    pc = 2 * g + k
                    V("dve", lambda: nc.vector.scalar_tensor_tensor(out=mixo[:, 2 + pc, :N], in0=ysb[pc].ap(0, N), scalar=ptab[:, cn_ + pc:cn_ + pc + 1], op0=MUL, in1=rs[:, :N], op1=MUL),
                      reads=ysb[pc].all() + rs.all() + ptab.all(), writes=[mixo.b(2 + pc)])

        def mixer_dn(l, N, nch, ti):
            inproj(l, N, ti, [(7, [(0, 1, True), (2, 3, True)]), (8, [(4, 5, True), (6, 7, True)]),
                              (9, [(8, True), (9, True)]), (10, [(10, False), (11, False)])], 10)
            if "dn" not in cfg.mixers:
                return
            ones64b = cb[0:64, CB["blk"][0]:CB["blk"][0] + 64]
            for j in range(8):
                conv4(l, "dn_cw", 10, j, j, 64, N, hc[:, j, :N], [hc.b(j)])
                V("act", lambda: nc.scalar.activation(out=hc[:, j, :N], in_=hc[:, j, :N], func=AF.Silu), reads=[hc.b(j)], writes=[hc.b(j)])
                q = rot("sqb", sqb)
                V("act", lambda: nc.scalar.activation(out=q[0:64, :N], in_=hc[:, j, :N], func=AF.Square), reads=[hc.b(j)], writes=q.all())
                pr = bank()
                V("pe", lambda: nc.tensor.matmul(pr[0:64, :N], ones64b, q[0:64, :N], start=True, stop=True), reads=q.all() + cb.all(), writes=pr.all())
                rs = tmp[j % 2]
                rstd_from(pr[0:64, :N], pr.all(), RMS_EPS, rs, N, M=64)
                if j < 4:
                    V("dve", lambda: nc.vector.scalar_tensor_tensor(out=qnb[:, j, :N], in0=hc[:, j, :N], scalar=0.125, op0=MUL, in1=rs[0:64, :N], op1=MUL),
                      reads=[hc.b(j)] + rs.all(), writes=qnb.all())
                else:
                    V("dve", lambda: nc.vector.tensor_tensor(out=hc[:, j, :N], in0=hc[:, j, :N], in1=rs[0:64, :N], op=MUL), reads=[hc.b(j)] + rs.all(), writes=[hc.b(j)])
                    V("act", lambda: nc.scalar.copy(out=knb[:, j - 4, :N], in_=hc[:, j, :N]), reads=[hc.b(j)], writes=knb.all())
            for j in range(2):
                o = cv[j]
                conv4(l, "dn_cw", 10, 8 + j, 8 + j, 128, N, o.ap(0, N), o.all())
                V("act", lambda: nc.scalar.activation(out=o.ap(0, N), in_=o.ap(0, N), func=AF.Silu), reads=o.all(), writes=o.all())
            if DBG_STOP == 21:
                return
            S = dS[l]
            id64 = cf[0:64, 0:64]
            for c in range(nch):
                cs_ = slice(c * 64, (c + 1) * 64)
                par = c % 2
                bk = bank()
                for hh in range(4):
                    V("pe", lambda hh=hh: nc.tensor.transpose(bk[0:64, hh * 64:(hh + 1) * 64], hc[:, 4 + hh, cs_], id64), reads=[hc.b(4 + hh)] + cf.all(), writes=bk.all())
                for pc in range(2):
                    V("pe", lambda pc=pc: nc.tensor.transpose(bk[0:64, 256 + pc * 128:256 + (pc + 1) * 128], cv[pc].ap(c * 64, (c + 1) * 64), cf[:, 0:128]), reads=cv[pc].all(), writes=bk.all())
                kb_, kd_, vb_ = kbd[par], kdec[par], vb[par]
                bkk = bk[0:64, 0:256].rearrange("p (e q) -> p e q", q=64)
                bkv = bk[0:64, 256:512].rearrange("p (e q) -> p e q", q=64)
                V("dve", lambda: nc.vector.tensor_tensor(out=kb_[:, :, :], in0=bkk, in1=BS[:, c, :].unsqueeze(2).broadcast_to([64, 4, 64]), op=MUL), reads=bk.all() + BS.all(), writes=kb_.all())
                V("dve", lambda: nc.vector.tensor_tensor(out=kd_[:, :, :], in0=bkk, in1=EX[:, c, 20:24].unsqueeze(2).broadcast_to([64, 4, 64]), op=MUL), reads=bk.all() + EX.all(), writes=kd_.all())
                V("dve", lambda: nc.vector.tensor_tensor(out=vb_[:, :, :], in0=bkv, in1=SC[:, c, 20:24].unsqueeze(2).broadcast_to([64, 4, 64]), op=MUL), reads=bk.all() + SC.all(), writes=vb_.all())
                if DBG_STOP == 22:
                    return
                V("dve", lambda: nc.vector.tensor_tensor(out=GUd2[:, :, :], in0=cfs("U2").unsqueeze(1).broadcast_to([64, 4, 64]), in1=SC[:, c, 16:20].unsqueeze(2).broadcast_to([64, 4, 64]), op=MUL),
                  reads=cf.all() + SC.all(), writes=GUd2.all())
                gdf = GUd2[:, :, :].rearrange("p e q -> p (e q)")
                bdd = bank()
                bee = bank()
                P.mm(bdd[0:64, 0:256], [(cfs("U1"), gdf), (id64, cf[0:64, CF["NEGi8"][0]:CF["NEGi8"][0] + 256])], reads=GUd2.all() + cf.all(), writes=bdd.all())
                for hh in range(4):
                    P.mm(bdd[0:64, 256 + hh * 64:256 + (hh + 1) * 64], [(GUd2[:, hh, :], cfs("U1")), (id64, cfs("NEGl"))], reads=GUd2.all() + cf.all(), writes=bdd.all())
                V("pe", lambda: nc.tensor.matmul(bee[0:64, 0:256], cfs("ones"), gdf, start=True, stop=True), reads=GUd2.all() + cf.all(), writes=bee.all())
                V("act", lambda: nc.scalar.activation(out=DEC[:, :, :], in_=bdd[0:64, :].rearrange("p (e q) -> p e q", q=64), func=AF.Exp), reads=bdd.all(), writes=DEC.all())
                V("act", lambda: nc.scalar.activation(out=EBd[:, :, :], in_=bee[0:64, 0:256].rearrange("p (e q) -> p e q", q=64), func=AF.Exp), reads=bee.all(), writes=EBd.all())
                V("dve", lambda: nc.vector.tensor_tensor(out=DECb[:, :, :], in0=DEC[:, 4:8, :], in1=SC[:, c, 20:24].unsqueeze(2).broadcast_to([64, 4, 64]), op=MUL), reads=DEC.all() + SC.all(), writes=DECb.all())
                if DBG_STOP == 23:
                    return
                bkk2 = bank()
                for hh in range(4):
                    kT = knb[:, hh, cs_]
                    V("pe", lambda hh=hh, kT=kT: nc.tensor.matmul(bkk2[0:64, hh * 64:(hh + 1) * 64], kT, kT, start=True, stop=True), reads=knb.all(), writes=bkk2.all())
                    V("pe", lambda hh=hh, kT=kT: nc.tensor.matmul(bkk2[0:64, 256 + hh * 64:256 + (hh + 1) * 64], kT, qnb[:, hh, cs_], start=True, stop=True), reads=knb.all() + qnb.all(), writes=bkk2.all())
                at_, qd_ = attnT[par], qdT[par]
                V("dve", lambda: nc.vector.tensor_tensor(out=mf[:, :, :], in0=bkk2[0:64, 0:256].rearrange("p (e q) -> p e q", q=64), in1=DECb[:, :, :], op=MUL), reads=bkk2.all() + DECb.all(), writes=mf.all())
                V("dve", lambda: nc.vector.tensor_tensor(out=at_[:, :, :], in0=bkk2[0:64, 256:512].rearrange("p (e q) -> p e q", q=64), in1=DEC[:, 0:4, :], op=MUL), reads=bkk2.all() + DEC.all(), writes=at_.all())
                V("dve", lambda: nc.vector.tensor_tensor(out=qd_[:, :, :], in0=qnb[:, :, cs_], in1=EBd[:, :, :], op=MUL), reads=qnb.all() + EBd.all(), writes=qd_.all())
                if DBG_STOP == 24:
                    return
                bn = bank()
                for hh in range(4):
                    if DBG_SKIP & 1:
                        continue
                    V("pe", lambda hh=hh: nc.tensor.transpose(bn[0:64, hh * 64:(hh + 1) * 64], mf[:, hh, :], id64), reads=mf.all() + cf.all(), writes=bn.all())
                pq0 = PQ[0]
                x0 = Xk[0]
                bnv = bn[0:64, 0:256].rearrange("p (e q) -> p e q", q=64)
                if not (DBG_SKIP & 2):
                    V("dve", lambda: nc.vector.tensor_copy(out=pq0[:, :, 0:64], in_=bnv), reads=bn.all(), writes=pq0.all())
                if not (DBG_SKIP & 4):
                    V("act", lambda: nc.scalar.copy(out=pq0[:, :, 64:128], in_=mf[:, :, :]), reads=mf.all(), writes=pq0.all())
                V("dve", lambda: nc.vector.tensor_tensor(out=x0[:, :, :], in0=id64.unsqueeze(1).broadcast_to([64, 4, 64]), in1=bnv, op=SUB),
                  reads=bn.all() + cf.all(), writes=x0.all())
                if DBG_STOP == 241:
                    return
                pqc, xc = pq0, x0
                for st in range(6):
                    if DBG_STOP == 250 + st:
                        return
                    pqn, xn = PQ[(st + 1) % 2], Xk[(st + 1) % 2]
                    if st < 5:
                        bp = bank()
                        for hh in range(4):
                            Pk, Qk = pqc[:, hh, 0:64], pqc[:, hh, 64:128]
                            V("pe", lambda: nc.tensor.matmul(bp[0:64, hh * 128:hh * 128 + 64], Qk, Pk, start=True, stop=True), reads=pqc.all(), writes=bp.all())
                            V("pe", lambda: nc.tensor.matmul(bp[0:64, hh * 128 + 64:hh * 128 + 128], Pk, Qk, start=True, stop=True), reads=pqc.all(), writes=bp.all())
                        V("act", lambda: nc.scalar.copy(out=pqn[:, :, :], in_=bp[0:64, :].rearrange("p (e q) -> p e q", q=128)), reads=bp.all(), writes=pqn.all())
                    if st >= 1:
                        bxd = bank()
                        for hh in range(4):
                            V("pe", lambda: nc.tensor.matmul(bxd[0:64, hh * 64:(hh + 1) * 64], pqc[:, hh, 64:128], xc[:, hh, :], start=True, stop=True),
                              reads=pqc.all() + xc.all(), writes=bxd.all())
                        V("dve", lambda: nc.vector.tensor_tensor(out=xn[:, :, :], in0=xc[:, :, :], in1=bxd[0:64, 0:256].rearrange("p (e q) -> p e q", q=64), op=ADD),
                          reads=xc.all() + bxd.all(), writes=xn.all())
                        xc = xn
                    pqc = pqn
                TT = xc
                if DBG_STOP == 25:
                    return
                bu = bank()
                for hh in range(4):
                    V("pe", lambda: nc.tensor.matmul(bu[0:64, hh * 64:(hh + 1) * 64], TT[:, hh, :], vb_[:, hh, :], start=True, stop=True), reads=TT.all() + vb_.all(), writes=bu.all())
                    V("pe", lambda: nc.tensor.matmul(bu[0:64, 256 + hh * 64:256 + (hh + 1) * 64], kb_[:, hh, :], TT[:, hh, :], start=True, stop=True), reads=TT.all() + kb_.all(), writes=bu.all())
                wt = wTb[par]
                V("act", lambda: nc.scalar.copy(out=usb[:, :, :], in_=bu[0:64, 0:256].rearrange("p (e q) -> p e q", q=64)), reads=bu.all(), writes=usb.all())
                V("act", lambda: nc.scalar.copy(out=wt[:, :, :], in_=bu[0:64, 256:512].rearrange("p (e q) -> p e q", q=64)), reads=bu.all(), writes=wt.all())
                if DBG_STOP == 26:
                    return
                sb_ = Sb[par]
                V("act", lambda: nc.scalar.copy(out=sb_[:, :, :], in_=S[:, :, :]), reads=S.all(), writes=sb_.all())
                bv = bank()
                for hh in range(4):
                    V("pe", lambda: nc.tensor.matmul(bv[0:64, hh * 64:(hh + 1) * 64], wt[:, hh, :], sb_[:, hh, :], start=True, stop=True), reads=wt.all() + sb_.all(), writes=bv.all())
                vn = vnew[par]
                V("dve", lambda: nc.vector.tensor_tensor(out=vn[:, :, :], in0=usb[:, :, :], in1=bv[0:64, 0:256].rearrange("p (e q) -> p e q", q=64), op=SUB), reads=usb.all() + bv.all(), writes=vn.all())
                bo = bank()
                bs2 = bank()
                for hh in range(4):
                    pc, hp = hh // 2, hh % 2
                    P.mm(bo[64 * hp:64 * hp + 64, pc * 64:(pc + 1) * 64], [(vn[:, hh, :], at_[:, hh, :]), (sb_[:, hh, :], qd_[:, hh, :])],
                         reads=vn.all() + at_.all() + sb_.all() + qd_.all(), writes=bo.all())
                for hh in range(4):
                    V("pe", lambda: nc.tensor.matmul(bs2[0:64, hh * 64:(hh + 1) * 64], kd_[:, hh, :], vn[:, hh, :], start=True, stop=True), reads=kd_.all() + vn.all(), writes=bs2.all())
                for pc in range(2):
                    V("act", lambda: nc.scalar.copy(out=ysb[pc].ap(c * 64, (c + 1) * 64), in_=bo[:, pc * 64:(pc + 1) * 64]), reads=bo.all(), writes=ysb[pc].all())
                V("dve", lambda: nc.vector.tensor_tensor(out=S[:, :, :], in0=S[:, :, :], in1=EBd[:, :, 63:64].broadcast_to([64, 4, 64]), op=MUL), reads=S.all() + EBd.all(), writes=S.all())
                V("dve", lambda: nc.vector.tensor_tensor(out=S[:, :, :], in0=S[:, :, :], in1=bs2[0:64, 0:256].rearrange("p (e q) -> p e q", q=64), op=ADD), reads=S.all() + bs2.all(), writes=S.all())
            cn_ = pcols[("dn_nw", l)]
            for pc in range(2):
                q = rot("sqb", sqb)
                V("act", lambda: nc.scalar.activation(out=q[:, :N], in_=ysb[pc].ap(0, N), func=AF.Square), reads=ysb[pc].all(), writes=q.all())
                pr = bank()
                V("pe", lambda: nc.tensor.matmul(pr[:, :N], cbs("blk"), q[:, :N], start=True, stop=True), reads=q.all() + cb.all(), writes=pr.all())
                rs = tmp[pc]
                rstd_from(pr[:, :N], pr.all(), RMS_EPS, rs, N, scale=1.0 / 64.0)
                V("act", lambda: nc.scalar.activation(out=zin[:, 10 + pc, 3:3 + N], in_=zin[:, 10 + pc, 3:3 + N], func=AF.Silu), reads=[zin.b(10 + pc)], writes=[zin.b(10 + pc)])
                V("dve", lambda: nc.vector.scalar_tensor_tensor(out=ysb[pc].ap(0, N), in0=ysb[pc].ap(0, N), scalar=ptab[:, cn_:cn_ + 1], op0=MUL, in1=rs[:, :N], op1=MUL),
                  reads=ysb[pc].all() + rs.all() + ptab.all(), writes=ysb[pc].all())
                V("dve", lambda: nc.vector.tensor_tensor(out=mixo[:, 6 + pc, :N], in0=ysb[pc].ap(0, N), in1=zin[:, 10 + pc, 3:3 + N], op=MUL), reads=ysb[pc].all() + [zin.b(10 + pc)], writes=[mixo.b(6 + pc)])

        def mixer(l, N, nch, ti, t0):
            small_scalars(l, N, nch, ti)
            mixer_lru(l, N, ti)
            mixer_ssd(l, N, nch, ti)
            mixer_dn(l, N, nch, ti)
            if cfg.dbg_mix:
                for c in range(NKC):
                    V("act", lambda c=c: nc.scalar.copy(out=h[:, c, :N], in_=mixo[:, c, :N]), reads=[mixo.b(c)], writes=[h.b(c)])
                for j in range(4):
                    wnext("out", j, l)
                return
            for j in range(4):
                w = wnext("out", j, l)
                for cc in range(2):
                    o = 2 * j + cc
                    pd = bank()
                    P.mm(pd[:, :N], [(w[:, (cc * NKC + kc) * 128:(cc * NKC + kc + 1) * 128], mixo[:, kc, :N]) for kc in range(NKC)],
                         reads=w.all() + mixo.all(), writes=pd.all())
                    V("dve", lambda pd=pd, o=o: nc.vector.scalar_tensor_tensor(out=h[:, o, :N], in0=h[:, o, :N], scalar=ALPHA, op0=MUL, in1=pd[:, :N], op1=ADD),
                      reads=[h.b(o)] + pd.all(), writes=[h.b(o)])
            layer_norm(l, 1, N)

        t0 = 0
        xTv = xT.rearrange("(c p) t -> p c t", p=128)
        yTv = yT.rearrange("(c p) t -> p c t", p=128)
        for ti, nch in enumerate(cfg.tile_chunks):
            N = nch * CH
            V("sp", lambda: nc.sync.dma_start(out=h[:, :, :N], in_=xTv[:, :, t0:t0 + N]), writes=h.all(), chan=io_ch)
            for c in range(NKC):
                V("act", lambda c=c: nc.scalar.copy(out=hbf[:, c, :N], in_=h[:, c, :N]), reads=[h.b(c)], writes=[hbf.b(c)])
            for l in range(cfg.n_layers):
                if "ffn1" in cfg.stages:
                    ffn(l, 0, N)
                if mix_on:
                    mixer(l, N, nch, ti, t0)
                if "ffn2" in cfg.stages:
                    ffn(l, 1, N)
            V("sp", lambda: nc.sync.dma_start(out=yTv[:, :, t0:t0 + N], in_=h[:, :, :N]), reads=h.all(), chan=io_ch)
            t0 += N
        P.wait_all("sp", io_ch)
        cfg.counts = {k: (v.epoch, v.cnt) for k, v in P.src.items()}
        cfg.ninst = P.ninst
    return nc


_CACHE = {}


def kernel(**inputs):
    inp = {k: np.asarray(v) for k, v in inputs.items()}
    x = inp["x"].astype(np.float32, copy=False)
    B = x.shape[0]
    cfg = Cfg()
    if "nc" not in _CACHE:
        _CACHE["nc"] = build(cfg)
    nc = _CACHE["nc"]
    wst = [host_stream(inp, l, cfg.stages) for l in range(DEPTH)]
    ptab = host_ptab(inp)
    rtab = host_rtab(inp)
    cf, cb = host_consts()
    metaT = np.ascontiguousarray(inp["meta"].astype(np.float32).T)
    in_maps = []
    for b in range(B):
        xT = np.zeros((D_MODEL, PADT), np.float32)
        xT[:, CH - N_META:CH] = metaT
        xT[:, CH:] = x[b].T
        m = {"xT": xT, "ptab": ptab, "rtab": rtab, "cf": cf, "cb": cb}
        for l in range(DEPTH):
            m["wst%d" % l] = wst[l]
        in_maps.append(m)
    res = run_bass_kernel_spmd(nc, in_maps, core_ids=list(range(B)))
    out = np.empty((B, SEQ, D_MODEL), np.float32)
    for b in range(B):
        out[b] = res.results[b]["yT"][:, CH:].T
    return out
```
